# Optimizing a Trainium2 kernel written in Bass

```python
import jax, jax.numpy as jnp
from jax import lax
import numpy as np

D_MODEL = 2048
BATCH = 2
SEQ = 4096
DEPTH = 4

CHUNK = 64
N_MIXERS = 2
N_CONV_LAYERS = (DEPTH + N_MIXERS - 1) // N_MIXERS
N_SSM_LAYERS = DEPTH // N_MIXERS

CONV_KERNEL = 31

SSM_EXPAND = 2
SSM_D_INNER = SSM_EXPAND * D_MODEL
SSM_HEAD_DIM = 64
SSM_N_HEADS = SSM_D_INNER // SSM_HEAD_DIM
SSM_N_GROUPS = 8
SSM_HEADS_PER_GROUP = SSM_N_HEADS // SSM_N_GROUPS
SSM_D_STATE = 128
SSM_CONV_KERNEL = 4
SSM_CONV_DIM = SSM_D_INNER + 2 * SSM_N_GROUPS * SSM_D_STATE
SSM_IN_DIM = SSM_D_INNER + SSM_CONV_DIM + SSM_N_HEADS

FFN_HIDDEN = 5632
FFN_CONV_KERNEL = 3

RMS_EPS = 1e-6
LN_EPS = 1e-5

kernel_name = "hybrid_conformer_mamba2_convffn_trunk"


def rms_norm(x, g, eps=RMS_EPS):
    xf = x.astype(jnp.float32)
    y = xf * lax.rsqrt(jnp.mean(xf * xf, axis=-1, keepdims=True) + eps)
    return (y * g.astype(jnp.float32)).astype(x.dtype)


def layer_norm(x, g, b, eps=LN_EPS):
    xf = x.astype(jnp.float32)
    mu = jnp.mean(xf, axis=-1, keepdims=True)
    xc = xf - mu
    var = jnp.mean(xc * xc, axis=-1, keepdims=True)
    y = xc * lax.rsqrt(var + eps) * g.astype(jnp.float32) + b.astype(jnp.float32)
    return y.astype(x.dtype)


def causal_depthwise_conv(x, w, b):
    k, c = w.shape
    xp = jnp.pad(x, ((0, 0), (k - 1, 0), (0, 0)))
    y = lax.conv_general_dilated(
        xp, w[:, None, :].astype(x.dtype), window_strides=(1,), padding="VALID",
        dimension_numbers=("NWC", "WIO", "NWC"), feature_group_count=c)
    return y + b.astype(x.dtype)


def conformer_conv_module(h, w_in, b_in, w_dw, b_dw, ln_g, ln_b, w_out, b_out):
    u = h @ w_in + b_in
    a, gate = jnp.split(u, 2, axis=-1)
    v = a * jax.nn.sigmoid(gate)
    v = causal_depthwise_conv(v, w_dw, b_dw)
    v = jax.nn.silu(layer_norm(v, ln_g, ln_b))
    return v @ w_out + b_out


def segsum(a):
    q = a.shape[-1]
    a_rep = jnp.broadcast_to(a[..., :, None], a.shape + (q,))
    strict = jnp.tril(jnp.ones((q, q), dtype=bool), -1)
    ss = jnp.cumsum(jnp.where(strict, a_rep, 0.0), axis=-2)
    return jnp.where(jnp.tril(jnp.ones((q, q), dtype=bool)), ss, -jnp.inf)


def ssd_chunked(x, dt, a_neg, bm, cm):
    bsz, seq, _, _ = x.shape
    nc = seq // CHUNK
    g, r, p, n = SSM_N_GROUPS, SSM_HEADS_PER_GROUP, SSM_HEAD_DIM, SSM_D_STATE
    xd = (x * dt[..., None]).reshape(bsz, nc, CHUNK, g, r, p)
    a = jnp.moveaxis((dt * a_neg).reshape(bsz, nc, CHUNK, g, r), 2, -1)
    bc = bm.reshape(bsz, nc, CHUNK, g, n)
    cc = cm.reshape(bsz, nc, CHUNK, g, n)
    a_cs = jnp.cumsum(a, axis=-1)
    decay_in = jnp.exp(segsum(a))
    cb = jnp.einsum("bclgn,bcsgn->bcgls", cc, bc)
    y_diag = jnp.einsum("bcgls,bcgrls,bcsgrp->bclgrp", cb, decay_in, xd)
    decay_to_end = jnp.exp(a_cs[..., -1:] - a_cs)
    states = jnp.einsum("bcsgn,bcgrs,bcsgrp->bcgrpn", bc, decay_to_end, xd)
    chunk_decay = jnp.exp(a_cs[..., -1])

    def step(carry, inp):
        st, dec = inp
        return carry * dec[..., None, None] + st, carry

    init = jnp.zeros((bsz, g, r, p, n), dtype=x.dtype)
    _, prev = lax.scan(step, init, (jnp.moveaxis(states, 1, 0), jnp.moveaxis(chunk_decay, 1, 0)))
    prev = jnp.moveaxis(prev, 0, 1)
    y_off = jnp.einsum("bclgn,bcgrpn,bcgrl->bclgrp", cc, prev, jnp.exp(a_cs))
    return (y_diag + y_off).reshape(bsz, seq, g * r, p)


def gated_group_rms_norm(y, z, g):
    yf = (y * jax.nn.silu(z)).astype(jnp.float32)
    shp = yf.shape
    yg = yf.reshape(shp[:-1] + (SSM_N_GROUPS, shp[-1] // SSM_N_GROUPS))
    yg = yg * lax.rsqrt(jnp.mean(yg * yg, axis=-1, keepdims=True) + RMS_EPS)
    return (yg.reshape(shp) * g.astype(jnp.float32)).astype(z.dtype)


def mamba2_mixer(h, w_in, w_conv, b_conv, dt_bias, a_log, d_skip, norm_g, w_out):
    bsz, seq, _ = h.shape
    zxbcdt = h @ w_in
    z, xbc, dt = jnp.split(zxbcdt, [SSM_D_INNER, SSM_D_INNER + SSM_CONV_DIM], axis=-1)
    xbc = jax.nn.silu(causal_depthwise_conv(xbc, w_conv, b_conv))
    gn = SSM_N_GROUPS * SSM_D_STATE
    xs, bm, cm = jnp.split(xbc, [SSM_D_INNER, SSM_D_INNER + gn], axis=-1)
    xs = xs.reshape(bsz, seq, SSM_N_HEADS, SSM_HEAD_DIM).astype(jnp.float32)
    bm = bm.reshape(bsz, seq, SSM_N_GROUPS, SSM_D_STATE).astype(jnp.float32)
    cm = cm.reshape(bsz, seq, SSM_N_GROUPS, SSM_D_STATE).astype(jnp.float32)
    dt = jax.nn.softplus(dt.astype(jnp.float32) + dt_bias.astype(jnp.float32))
    a_neg = -jnp.exp(a_log.astype(jnp.float32))
    y = ssd_chunked(xs, dt, a_neg, bm, cm)
    y = y + d_skip.astype(jnp.float32)[:, None] * xs
    y = y.reshape(bsz, seq, SSM_D_INNER).astype(h.dtype)
    return gated_group_rms_norm(y, z, norm_g) @ w_out


def conv_ffn(h, w_up, w_dw, b_dw, w_down):
    u = causal_depthwise_conv(h @ w_up, w_dw, b_dw)
    gate, val = jnp.split(u, 2, axis=-1)
    return (jax.nn.silu(gate) * val) @ w_down


def setup_inputs(seed: int = 0) -> dict:
    key = jax.random.key(seed)
    ks = jax.random.split(key, 32)
    d, f = D_MODEL, FFN_HIDDEN
    nc, ns = N_CONV_LAYERS, N_SSM_LAYERS

    def nrm(k, shape, scale):
        return jax.random.normal(k, shape, dtype=jnp.float32) * scale

    dt0 = jnp.exp(jax.random.uniform(ks[16], (ns, SSM_N_HEADS), minval=math_log(1e-3), maxval=math_log(1e-1)))
    return {
        "x": nrm(ks[0], (BATCH, SEQ, d), 1.0),
        "norm_mix_g": 1.0 + nrm(ks[1], (DEPTH, d), 0.05),
        "norm_ffn_g": 1.0 + nrm(ks[2], (DEPTH, d), 0.05),
        "norm_final_g": 1.0 + nrm(ks[3], (d,), 0.05),
        "cv_w_in": nrm(ks[4], (nc, d, 2 * d), d ** -0.5),
        "cv_b_in": nrm(ks[5], (nc, 2 * d), 0.02),
        "cv_w_dw": nrm(ks[6], (nc, CONV_KERNEL, d), CONV_KERNEL ** -0.5),
        "cv_b_dw": nrm(ks[7], (nc, d), 0.02),
        "cv_ln_g": 1.0 + nrm(ks[8], (nc, d), 0.05),
        "cv_ln_b": nrm(ks[9], (nc, d), 0.02),
        "cv_w_out": nrm(ks[10], (nc, d, d), d ** -0.5),
        "cv_b_out": nrm(ks[11], (nc, d), 0.02),
        "ssm_w_in": nrm(ks[12], (ns, d, SSM_IN_DIM), d ** -0.5),
        "ssm_w_conv": nrm(ks[13], (ns, SSM_CONV_KERNEL, SSM_CONV_DIM), SSM_CONV_KERNEL ** -0.5),
        "ssm_b_conv": nrm(ks[14], (ns, SSM_CONV_DIM), 0.02),
        "ssm_dt_bias": dt0 + jnp.log(-jnp.expm1(-dt0)),
        "ssm_a_log": jnp.log(jax.random.uniform(ks[17], (ns, SSM_N_HEADS), minval=1.0, maxval=16.0)),
        "ssm_d": 1.0 + nrm(ks[18], (ns, SSM_N_HEADS), 0.1),
        "ssm_norm_g": 1.0 + nrm(ks[19], (ns, SSM_D_INNER), 0.05),
        "ssm_w_out": nrm(ks[20], (ns, SSM_D_INNER, d), SSM_D_INNER ** -0.5),
        "ffn_w_up": nrm(ks[21], (DEPTH, d, 2 * f), d ** -0.5),
        "ffn_w_dw": nrm(ks[22], (DEPTH, FFN_CONV_KERNEL, 2 * f), FFN_CONV_KERNEL ** -0.5),
        "ffn_b_dw": nrm(ks[23], (DEPTH, 2 * f), 0.02),
        "ffn_w_down": nrm(ks[24], (DEPTH, f, d), f ** -0.5),
    }


def math_log(v):
    return float(np.log(v))


def reference(x, norm_mix_g, norm_ffn_g, norm_final_g,
              cv_w_in, cv_b_in, cv_w_dw, cv_b_dw, cv_ln_g, cv_ln_b, cv_w_out, cv_b_out,
              ssm_w_in, ssm_w_conv, ssm_b_conv, ssm_dt_bias, ssm_a_log, ssm_d, ssm_norm_g, ssm_w_out,
              ffn_w_up, ffn_w_dw, ffn_b_dw, ffn_w_down):
    for i in range(DEPTH):
        h = rms_norm(x, norm_mix_g[i])
        j = i // N_MIXERS
        if i % N_MIXERS == 0:
            x = x + conformer_conv_module(h, cv_w_in[j], cv_b_in[j], cv_w_dw[j], cv_b_dw[j],
                                          cv_ln_g[j], cv_ln_b[j], cv_w_out[j], cv_b_out[j])
        else:
            x = x + mamba2_mixer(h, ssm_w_in[j], ssm_w_conv[j], ssm_b_conv[j], ssm_dt_bias[j],
                                 ssm_a_log[j], ssm_d[j], ssm_norm_g[j], ssm_w_out[j])
        x = x + conv_ffn(rms_norm(x, norm_ffn_g[i]), ffn_w_up[i], ffn_w_dw[i], ffn_b_dw[i], ffn_w_down[i])
    return rms_norm(x, norm_final_g)
```

```python
import os
import numpy as np
import ml_dtypes
import concourse.bass as bass
import concourse.mybir as mybir
from concourse.bass_utils import run_bass_kernel_spmd

F32 = mybir.dt.float32
BF16 = mybir.dt.bfloat16
AF = mybir.ActivationFunctionType
ALU = mybir.AluOpType

NCORES = 8
D = 2048
KC = 16
T = 1024
HALO = 32
DEPTH = 4
FF = 5632
FCH = 44
CONVK = 31
DI = 4096
NH = 64
NG = 8
DSTATE = 128
SSM_IN = 10304
RMS_EPS = 1e-6
LN_EPS = 1e-5
NST = 3
NSL = 6
WBLK = 2048
PF_CAST = 2
PF_DMA = 4
SBUF_LIMIT = 229344


class Res:
    __slots__ = ("name", "lw", "rd")

    def __init__(self, name):
        self.name = name
        self.lw = None
        self.rd = []


class Chan:
    __slots__ = ("sem", "n", "unit")

    def __init__(self, sem, unit=16):
        self.sem = sem
        self.n = 0
        self.unit = unit


class Op:
    __slots__ = ("fn", "sig", "sigval", "waits", "chan")

    def __init__(self, fn, chan):
        self.fn = fn
        self.sig = False
        self.sigval = 0
        self.waits = []
        self.chan = chan


ENGS = ("pe", "act", "dve", "pool", "sp")
ENGATTR = {"pe": "tensor", "act": "scalar", "dve": "vector", "pool": "gpsimd", "sp": "sync"}


class Prog:
    def __init__(self):
        self.ops = {e: [] for e in ENGS}
        self.waited = {e: {} for e in ENGS}
        self.pending = {e: {} for e in ENGS}
        self.chans = []

    def op(self, eng, fn, r=(), w=(), chan=None):
        deps = set()
        for x in r:
            if x.lw is not None:
                deps.add(x.lw)
        for x in w:
            if x.lw is not None:
                deps.add(x.lw)
            deps.update(x.rd)
        if chan is not None and chan.n > 0:
            deps.add(("c", chan, chan.n))
        best = dict(self.pending[eng])
        self.pending[eng] = {}
        for d in deps:
            if d[0] == "e":
                if d[1] == "pe" and eng == "pe":
                    continue
                key, val = d[1], d[2] + 1
            else:
                key, val = d[1], d[2]
            if best.get(key, 0) < val:
                best[key] = val
        o = Op(fn, chan)
        wd = self.waited[eng]
        for key, val in best.items():
            if wd.get(key, 0) >= val:
                continue
            wd[key] = val
            o.waits.append((key, val))
            if isinstance(key, str):
                self.ops[key][val - 1].sig = True
        idx = len(self.ops[eng])
        self.ops[eng].append(o)
        if chan is not None:
            chan.n += 1
            tok = ("c", chan, chan.n)
        else:
            tok = ("e", eng, idx)
        for x in r:
            x.rd.append(tok)
        for x in w:
            x.lw = tok
            x.rd = []
        return o

    def barrier(self):
        for e in ENGS:
            p = self.pending[e]
            for e2 in ENGS:
                if e2 == "pe" and e == "pe":
                    continue
                n = len(self.ops[e2])
                if n > 0 and self.ops[e2][n - 1].chan is None:
                    p[e2] = max(p.get(e2, 0), n)
                elif n > 0:
                    k = n
                    while k > 0 and self.ops[e2][k - 1].chan is not None:
                        k -= 1
                    if k > 0:
                        p[e2] = max(p.get(e2, 0), k)
            for c in self.chans:
                if c.n > 0:
                    p[c] = max(p.get(c, 0), c.n)

    def emit(self, block, sem_of_eng):
        for e in ENGS:
            c = 0
            for o in self.ops[e]:
                if o.sig:
                    c += 1
                o.sigval = c

        def make(e):
            def body(eng):
                mysem = sem_of_eng[e]
                for o in self.ops[e]:
                    for key, val in o.waits:
                        if isinstance(key, str):
                            eng.wait_ge(sem_of_eng[key], self.ops[key][val - 1].sigval)
                        else:
                            eng.wait_ge(key.sem, val * key.unit)
                    ins = o.fn(eng)
                    if o.chan is not None:
                        ins.then_inc(o.chan.sem, o.chan.unit)
                    elif o.sig:
                        ins.then_inc(mysem, 1)
            return body

        for e in ENGS:
            getattr(block, ENGATTR[e])(make(e))


def _fm(v):
    v = np.asarray(v, np.float32)
    return np.ascontiguousarray(v.reshape(-1, 128).T)


def _fm_taps(w):
    w = np.asarray(w, np.float32)
    k, f = w.shape
    return np.ascontiguousarray(w.T.reshape(f // 128, 128, k).transpose(1, 0, 2).reshape(128, -1))


def _rep(v):
    v = np.asarray(v, np.float32).reshape(1, -1)
    return np.ascontiguousarray(np.repeat(v, 128, axis=0))


class Pack:
    def __init__(self):
        self.cols = []
        self.off = {}
        self.n = 0

    def add(self, name, arr):
        self.off[name] = (self.n, arr.shape[1])
        self.cols.append(arr)
        self.n += arr.shape[1]

    def build(self):
        return np.ascontiguousarray(np.concatenate(self.cols, axis=1))


def core_flags(core):
    q = core % 4
    sel = np.zeros((128, 8), np.float32)
    if q > 0:
        sel[:, core - 1] = 1.0
    hp = np.full((128, 1), 1.0 if q > 0 else 0.0, np.float32)
    pm = np.zeros((128, 4), np.float32)
    pm[:, :q] = 1.0
    return sel, hp, pm


def pack_phase(inp, ph, core):
    kind, li = ph[0], ph[1]
    pk = Pack()
    sel, hp, pm = core_flags(core)
    pk.add("sel", sel)
    pk.add("has_prev", hp)
    pk.add("prev_mask", pm)
    if kind == "ffn":
        pk.add("norm_g", _fm(inp["norm_ffn_g"][li]))
        pk.add("w_dw", _fm_taps(inp["ffn_w_dw"][li]))
        pk.add("b_dw", _fm(inp["ffn_b_dw"][li]))
    elif kind == "conv":
        j = li // 2
        pk.add("norm_g", _fm(inp["norm_mix_g"][li]))
        pk.add("b_in", _fm(inp["cv_b_in"][j]))
        pk.add("w_dw", _fm_taps(inp["cv_w_dw"][j]))
        pk.add("b_dw", _fm(inp["cv_b_dw"][j]))
        pk.add("ln_g", _fm(inp["cv_ln_g"][j]))
        pk.add("ln_b", _fm(inp["cv_ln_b"][j]))
        pk.add("b_out", _fm(inp["cv_b_out"][j]))
    elif kind == "ssm":
        j = li // 2
        pk.add("norm_g", _fm(inp["norm_mix_g"][li]))
        pk.add("w_conv", _fm_taps(inp["ssm_w_conv"][j]))
        pk.add("b_conv", _fm(inp["ssm_b_conv"][j]))
        pk.add("dt_bias", _rep(inp["ssm_dt_bias"][j]))
        pk.add("a_log", _rep(inp["ssm_a_log"][j]))
        pk.add("d_skip", _rep(inp["ssm_d"][j]))
        pk.add("gn_g", _fm(inp["ssm_norm_g"][j]))
    elif kind == "final":
        pk.add("norm_g", _fm(inp["norm_final_g"]))
    return pk


PK_MAX = 640


def phase_weight_names(ph):
    kind, li = ph[0], ph[1]
    if kind == "ffn":
        return {"w_up": ("ffn_w_up", li), "w_down": ("ffn_w_down", li)}
    if kind == "conv":
        return {"w_in": ("cv_w_in", li // 2), "w_out": ("cv_w_out", li // 2)}
    if kind == "ssm":
        return {"w_in": ("ssm_w_in", li // 2), "w_out": ("ssm_w_out", li // 2)}
    return {}


WSHAPES = {
    ("ffn", "w_up"): [D, 2 * FF], ("ffn", "w_down"): [FF, D],
    ("conv", "w_in"): [D, 2 * D], ("conv", "w_out"): [D, D],
    ("ssm", "w_in"): [D, SSM_IN], ("ssm", "w_out"): [DI, D],
}


class Builder:
    def __init__(self, phases):
        self.phases = phases
        self.nc = bass.Bass("TRN2", target_bir_lowering=False)
        self.P = Prog()
        self.uid = 0

    def sb_at(self, name, shape, dt, off):
        self.uid += 1
        nbytes = int(np.prod(shape[1:])) * (2 if dt == BF16 else 4)
        assert self.arena_base + off + nbytes <= SBUF_LIMIT, (name, off, nbytes, self.arena_base)
        assert self.arena_base + off >= self.x_off
        return self.nc.alloc_sbuf_tensor_at(f"{name}_{self.uid}", shape, dt, offset=self.arena_base + off)

    def new_chan(self, name, unit=16):
        c = Chan(self.nc.alloc_semaphore(name), unit)
        self.P.chans.append(c)
        return c

    def nch(self):
        self.ch_rot_i += 1
        return self.ch_rot[self.ch_rot_i % len(self.ch_rot)]

    def pv(self, name, col=0, n=1):
        o, w = self.cur_off[name]
        assert col + n <= w, (name, col, n, w)
        return self.cur_pk[:, o + col:o + col + n]

    def pv_bound(self):
        pk, off = self.cur_pk, dict(self.cur_off)

        def pv(name, col=0, n=1):
            o, w = off[name]
            assert col + n <= w, (name, col, n, w)
            return pk[:, o + col:o + col + n]
        return pv

    class WStream:
        def __init__(self, b, blocks):
            self.b = b
            self.blocks = blocks
            self.n_dma = 0
            self.n_cast = 0
            self.base = b.wcount
            b.wcount += len(blocks)

        def advance(self, k):
            b = self.b
            P = b.P
            n = len(self.blocks)
            while self.n_cast < min(n, k + PF_CAST + 1) or self.n_dma < min(n, k + PF_DMA + 1):
                if self.n_dma < min(n, k + PF_DMA + 1) and self.n_dma - self.n_cast < NST:
                    i = self.n_dma
                    g = self.base + i
                    src = self.blocks[i]
                    nk, ncol = src.shape[1], src.shape[2]
                    st = b.wst[g % NST][:, 0:nk * ncol].rearrange("p (k n) -> p k n", k=nk)
                    P.op("sp", (lambda st=st, src=src: lambda e: e.dma_start(out=st, in_=src))(),
                         w=[b.R_ST[g % NST]], chan=b.ch_st[g % NST])
                    self.n_dma += 1
                else:
                    i = self.n_cast
                    g = self.base + i
                    src = self.blocks[i]
                    ne = src.shape[1] * src.shape[2]
                    st = b.wst[g % NST][:, 0:ne]
                    sl = b.wsl[g % NSL][:, 0:ne]
                    P.op("act", (lambda st=st, sl=sl: lambda e: e.activation(out=sl, in_=st, func=AF.Copy))(),
                         r=[b.R_ST[g % NST]], w=[b.R_SL[g % NSL]])
                    self.n_cast += 1

        def get(self, k):
            self.advance(k)
            g = self.base + k
            src = self.blocks[k]
            nk, ncol = src.shape[1], src.shape[2]
            v = self.b.wsl[g % NSL][:, 0:nk * ncol].rearrange("p (k n) -> p k n", k=nk)
            return v, self.b.R_SL[g % NSL]

    def build(self):
        nc = self.nc
        P = self.P
        nph = len(self.phases)
        self.d_xin = nc.dram_tensor("xin", [128, KC, T], F32, kind="ExternalInput").ap()
        self.d_xout = nc.dram_tensor("xout", [128, KC, T], F32, kind="ExternalOutput").ap()
        self.d_pk = nc.dram_tensor("params", [128, nph, PK_MAX], F32, kind="ExternalInput").ap()
        self.d_w = []
        for i, ph in enumerate(self.phases):
            dw = {}
            for wn in phase_weight_names(ph):
                dw[wn] = nc.dram_tensor(f"{wn}_{i}", WSHAPES[(ph[0], wn)], F32, kind="ExternalInput").ap()
            self.d_w.append(dw)
        cur = [(nc.sbuf_base + 63) // 64 * 64]

        def al(name, shape, dt):
            nbytes = int(np.prod(shape[1:])) * (2 if dt == BF16 else 4)
            off = cur[0]
            cur[0] += (nbytes + 63) // 64 * 64
            return nc.alloc_sbuf_tensor_at(name, shape, dt, offset=off), off
        self.x_res, self.x_off = al("x_res", [128, KC, T], F32)
        self.hb, self.hb_off = al("hb", [128, KC, HALO + T], BF16)
        self.pkb = [al(f"pk{i}", [128, PK_MAX], F32)[0] for i in range(2)]
        self.wst = [al(f"wst{i}", [128, WBLK], F32)[0] for i in range(NST)]
        self.wsl = [al(f"wsl{i}", [128, WBLK], BF16)[0] for i in range(NSL)]
        self.ones_bf = al("ones_bf", [128, 128], BF16)[0]
        self.ones_f = al("ones_f", [128, 128], F32)[0]
        self.has_ssm = any(p[0] == "ssm" for p in self.phases)
        if self.has_ssm:
            for nm in ("IDF", "TRIF", "SGT", "SAMEB", "UINC", "USGT", "SELLO", "SELHI"):
                setattr(self, nm, al(nm.lower(), [128, 128], F32)[0])
        self.arena_base = cur[0]
        self.psum = nc.alloc_psum_tensor("psum", [128, 8, 512], F32)
        self.R_X = [Res(f"x{k}") for k in range(KC)]
        self.R_HB = [Res(f"hb{k}") for k in range(KC)]
        self.R_HBH = Res("hbh")
        self.R_PK = [Res("pk0"), Res("pk1")]
        self.R_ST = [Res(f"st{i}") for i in range(NST)]
        self.R_SL = [Res(f"sl{i}") for i in range(NSL)]
        self.R_PS = [Res(f"ps{i}") for i in range(8)]
        self.R_PSH = [Res(f"psh{i}") for i in range(4)]
        self.R_CONST = Res("const")
        self.R_SND = Res("snd")
        self.R_RCV = Res("rcv")
        self.R_OUT = Res("out")
        self.ch_st = [self.new_chan(f"ch_st{i}") for i in range(NST)]
        self.ch_in = self.new_chan("ch_in")
        self.ch_pk = [self.new_chan("ch_pk0"), self.new_chan("ch_pk1")]
        self.ch_out = self.new_chan("ch_out")
        self.ch_snd = self.new_chan("ch_snd")
        self.ch_cc = self.new_chan("ch_cc", unit=1)
        self.ch_gat = self.new_chan("ch_gat")
        self.ch_rot = [self.new_chan(f"ch_rot{i}") for i in range(4)]
        self.ch_rot_i = 0
        self.wcount = 0
        sem_of_eng = {e: nc.alloc_semaphore(f"sem_{e}") for e in ENGS}
        P.op("sp", lambda e: e.dma_start(out=self.x_res[:], in_=self.d_xin), w=list(self.R_X), chan=self.ch_in)
        P.op("pool", lambda e: e.memset(self.ones_bf[:], 1.0), w=[self.R_CONST])
        P.op("pool", lambda e: e.memset(self.ones_f[:], 1.0), w=[self.R_CONST])
        if self.has_ssm:
            IDF, TRIF, SGT, SAMEB, UINC, USGT, SELLO, SELHI = (self.IDF, self.TRIF, self.SGT, self.SAMEB, self.UINC, self.USGT, self.SELLO, self.SELHI)
            R_CST = self.R_CONST
            G = "pool"
            P.op(G, lambda e: e.memset(IDF[:], 0.0), w=[R_CST])
            P.op(G, lambda e: e.affine_select(out=IDF[:], in_=IDF[:], pattern=[[-1, 128]], compare_op=ALU.not_equal, fill=1.0, base=0, channel_multiplier=1), r=[R_CST], w=[R_CST])
            P.op(G, lambda e: e.memset(TRIF[:], 1.0), r=[R_CST], w=[R_CST])
            P.op(G, lambda e: e.affine_select(out=TRIF[:], in_=TRIF[:], pattern=[[1, 128]], compare_op=ALU.is_ge, fill=0.0, base=0, channel_multiplier=-1), r=[R_CST], w=[R_CST])
            P.op(G, lambda e: e.memset(SGT[:], 1.0), r=[R_CST], w=[R_CST])
            P.op(G, lambda e: e.affine_select(out=SGT[:], in_=SGT[:], pattern=[[-1, 128]], compare_op=ALU.is_ge, fill=0.0, base=-1, channel_multiplier=1), r=[R_CST], w=[R_CST])
            P.op(G, lambda e: e.memset(SAMEB[:], 0.0), r=[R_CST], w=[R_CST])
            P.op(G, lambda e: e.memset(SAMEB[0:64, 0:64], 1.0), r=[R_CST], w=[R_CST])
            P.op(G, lambda e: e.memset(SAMEB[64:128, 64:128], 1.0), r=[R_CST], w=[R_CST])
            P.op("dve", lambda e: e.tensor_tensor(UINC[:], TRIF[:], SAMEB[:], ALU.mult), r=[R_CST], w=[R_CST])
            P.op("dve", lambda e: e.tensor_tensor(USGT[:], SGT[:], SAMEB[:], ALU.mult), r=[R_CST], w=[R_CST])
            P.op(G, lambda e: e.memset(SELLO[:], 0.0), r=[R_CST], w=[R_CST])
            P.op(G, lambda e: e.memset(SELLO[0:64, :], 1.0), r=[R_CST], w=[R_CST])
            P.op(G, lambda e: e.memset(SELHI[:], 0.0), r=[R_CST], w=[R_CST])
            P.op(G, lambda e: e.memset(SELHI[64:128, :], 1.0), r=[R_CST], w=[R_CST])
        for i, ph in enumerate(self.phases):
            self.pi = i
            self.cur_pk = self.pkb[i % 2]
            self.cur_pkres = self.R_PK[i % 2]
            P.op("sp", (lambda i=i: lambda e: e.dma_start(out=self.pkb[i % 2][:], in_=self.d_pk[:, i, :]))(),
                 w=[self.R_PK[i % 2]], chan=self.ch_pk[i % 2])
            self.cur_off = pack_phase(DUMMY_INP, ph, 0).off
            kind = ph[0]
            if kind == "ffn":
                self.phase_ffn()
            elif kind == "conv":
                self.phase_conv()
            elif kind == "ssm":
                self.phase_ssm()
            elif kind == "final":
                self.phase_final()
            else:
                raise ValueError(kind)
            P.barrier()
        if self.phases[-1][0] != "final":
            P.op("sp", lambda e: e.dma_start(out=self.d_xout, in_=self.x_res[:]), r=list(self.R_X), w=[self.R_OUT], chan=self.ch_out)
        P.op("sp", lambda e: e.nop(), r=[self.R_OUT])
        with nc.Block() as block:
            P.emit(block, sem_of_eng)
        return nc

    def norm(self, off_rs, off_sq, final_out=None):
        P = self.P
        rs = self.sb_at("rs", [128, T], F32, off_rs)
        sq = [self.sb_at(f"sq{i}", [128, 512], BF16, off_sq + i * 1024) for i in range(2)]
        R_RS = Res("rs")
        R_SQ = [Res("sq0"), Res("sq1")]
        ps = self.psum
        for t in range(2):
            cs = slice(t * 512, (t + 1) * 512)
            bank = 4 + t
            for kc in range(KC):
                b = kc % 2
                P.op("act", (lambda kc=kc, b=b, cs=cs: lambda e: e.activation(out=sq[b][:], in_=self.x_res[:, kc, cs], func=AF.Square))(),
                     r=[self.R_X[kc]], w=[R_SQ[b]])
                P.op("pe", (lambda kc=kc, b=b, bank=bank: lambda e: e.matmul(ps[:, bank, :], self.ones_bf[:], sq[b][:], start=(kc == 0), stop=(kc == KC - 1)))(),
                     r=[R_SQ[b], self.R_CONST], w=[self.R_PS[bank]])
            P.op("act", (lambda cs=cs, bank=bank: lambda e: e.activation(out=rs[:, cs], in_=ps[:, bank, :], func=AF.Sqrt, scale=1.0 / D, bias=RMS_EPS))(),
                 r=[self.R_PS[bank]], w=[R_RS])
        P.op("dve", lambda e: e.reciprocal(rs[:], rs[:]), r=[R_RS], w=[R_RS])
        gname = "norm_g"
        for kc in range(KC):
            gap = self.pv(gname, kc)
            if final_out is None:
                P.op("dve", (lambda kc=kc, gap=gap: lambda e: e.scalar_tensor_tensor(
                    out=self.hb[:, kc, HALO:HALO + T], in0=self.x_res[:, kc, :], scalar=gap, in1=rs[:],
                    op0=ALU.mult, op1=ALU.mult))(), r=[self.R_X[kc], R_RS, self.cur_pkres], w=[self.R_HB[kc]])
            else:
                P.op("dve", (lambda kc=kc, gap=gap: lambda e: e.scalar_tensor_tensor(
                    out=self.x_res[:, kc, :], in0=self.x_res[:, kc, :], scalar=gap, in1=rs[:],
                    op0=ALU.mult, op1=ALU.mult))(), r=[self.R_X[kc], R_RS, self.cur_pkres], w=[self.R_X[kc]])
                P.op("sp", (lambda kc=kc: lambda e: e.dma_start(out=self.d_xout[:, kc, :], in_=self.x_res[:, kc, :]))(),
                     r=[self.R_X[kc]], w=[self.R_OUT], chan=self.ch_out)

    def exchange_halo(self, nh, off_gat):
        P = self.P
        nc = self.nc
        self.uid += 1
        d_snd = nc.dram_tensor(f"snd{self.uid}", [128, KC * nh], BF16).ap()
        d_rcv = nc.dram_tensor(f"rcv{self.uid}", [NCORES * 128, KC * nh], BF16).ap()
        gat = self.sb_at("gat", [128, NCORES, KC * nh], BF16, off_gat)
        R_GAT = Res("gat")
        tail = self.hb[:, :, HALO + T - nh:HALO + T]
        P.op("sp", lambda e: e.dma_start(out=d_snd.rearrange("p (k h) -> p k h", k=KC), in_=tail),
             r=self.R_HB, w=[self.R_SND], chan=self.ch_snd)
        P.op("pool", lambda e: e.collective_compute("AllGather", ALU.bypass, replica_groups=[list(range(NCORES))],
                                                     ins=[d_snd], outs=[d_rcv]),
             r=[self.R_SND], w=[self.R_RCV], chan=self.ch_cc)
        P.op("sp", lambda e: e.dma_start(out=gat[:], in_=d_rcv.rearrange("(r p) n -> p r n", p=128)),
             r=[self.R_RCV], w=[R_GAT], chan=self.ch_gat)
        hh = self.hb[:, :, HALO - nh:HALO]
        for j in range(NCORES):
            gj = gat[:, j, :].rearrange("p (k h) -> p k h", k=KC)
            sj = self.pv("sel", j)
            if j == 0:
                P.op("dve", (lambda gj=gj, sj=sj: lambda e: e.tensor_scalar(hh, gj, sj, None, ALU.mult))(),
                     r=[R_GAT, self.cur_pkres], w=[self.R_HBH])
            else:
                P.op("dve", (lambda gj=gj, sj=sj: lambda e: e.scalar_tensor_tensor(
                    out=hh, in0=gj, scalar=sj, in1=hh, op0=ALU.mult, op1=ALU.add))(),
                     r=[R_GAT, self.cur_pkres, self.R_HBH], w=[self.R_HBH])

    def phase_ffn(self):
        P = self.P
        pv = self.pv_bound()
        ps = self.psum
        dw = self.d_w[self.pi]
        o = 0
        gq = self.sb_at("gq", [128, 11, T], BF16, o); o += 11 * T * 2
        ust = [self.sb_at(f"ust{i}", [128, HALO + T], F32, o + i * (HALO + T) * 4) for i in range(2)]; o += 2 * (HALO + T) * 4
        acc = [self.sb_at(f"acc{i}", [128, T], F32, o + i * T * 4) for i in range(3)]
        off_rs = o
        off_sq = o + T * 4
        off_gat = o + T * 4 + 2048
        o += 3 * T * 4
        R_GQ = [Res(f"gq{i}") for i in range(11)]
        R_UST = [Res("ust0"), Res("ust1")]
        R_USTH = [Res("usth0"), Res("usth1")]
        R_ACC = [Res(f"acc{i}") for i in range(3)]
        wup = dw["w_up"].rearrange("(k p) n -> p k n", p=128)
        wdn = dw["w_down"].rearrange("(k p) n -> p k n", p=128)
        blocks = []
        for qd in range(4):
            for jj in range(11):
                j = qd * 11 + jj
                blocks.append(wup[:, :, j * 128:(j + 1) * 128])
                blocks.append(wup[:, :, FF + j * 128:FF + (j + 1) * 128])
            for oc in range(KC):
                blocks.append(wdn[:, qd * 11:(qd + 1) * 11, oc * 128:(oc + 1) * 128])
        ws = Builder.WStream(self, blocks)
        ws.advance(0)
        self.norm(off_rs, off_sq)
        self.exchange_halo(2, off_gat)
        bi = 0
        unit = 0
        for qd in range(4):
            for jj in range(11):
                j = qd * 11 + jj
                accs = []
                for s in range(2):
                    wv, wres = ws.get(bi); bi += 1
                    u = unit; unit += 1
                    pa = 2 * (u % 3)
                    hb_ = 6 + (u % 2)
                    ub = u % 2
                    ab = u % 3
                    accs.append(ab)

                    def mm(e, wv=wv, pa=pa, hb_=hb_):
                        for kc in range(KC):
                            e.matmul(ps[:, pa, :], wv[:, kc, :], self.hb[:, kc, HALO:HALO + 512], start=(kc == 0), stop=(kc == KC - 1))
                            e.matmul(ps[:, pa + 1, :], wv[:, kc, :], self.hb[:, kc, HALO + 512:HALO + T], start=(kc == 0), stop=(kc == KC - 1))
                            ins = e.matmul(ps[:, hb_, 0:2], wv[:, kc, :], self.hb[:, kc, HALO - 2:HALO], start=(kc == 0), stop=(kc == KC - 1))
                        return ins
                    P.op("pe", mm, r=[wres, self.R_HBH] + self.R_HB, w=[self.R_PS[pa], self.R_PS[pa + 1], self.R_PS[hb_]])
                    P.op("act", (lambda ub=ub, pa=pa: lambda e: e.activation(
                        out=ust[ub][:, HALO:HALO + T].rearrange("p (a b) -> p a b", a=2), in_=ps[:, pa:pa + 2, :], func=AF.Copy))(),
                        r=[self.R_PS[pa], self.R_PS[pa + 1]], w=[R_UST[ub]])
                    P.op("act", (lambda ub=ub, hb_=hb_: lambda e: e.activation(
                        out=ust[ub][:, HALO - 2:HALO], in_=ps[:, hb_, 0:2], func=AF.Copy))(),
                        r=[self.R_PS[hb_]], w=[R_USTH[ub]])
                    ch = s * FCH + j
                    P.op("dve", (lambda ub=ub, ab=ab, ch=ch: lambda e: e.tensor_scalar(
                        acc[ab][:], ust[ub][:, HALO:HALO + T], pv("w_dw", ch * 3 + 2), pv("b_dw", ch), ALU.mult, ALU.add))(),
                        r=[R_UST[ub], self.cur_pkres], w=[R_ACC[ab]])
                    for k in (1, 0):
                        sh = 2 - k
                        P.op("dve", (lambda ub=ub, ab=ab, ch=ch, k=k, sh=sh: lambda e: e.scalar_tensor_tensor(
                            out=acc[ab][:], in0=ust[ub][:, HALO - sh:HALO - sh + T], scalar=pv("w_dw", ch * 3 + k), in1=acc[ab][:],
                            op0=ALU.mult, op1=ALU.add))(),
                            r=[R_UST[ub], R_USTH[ub], self.cur_pkres, R_ACC[ab]], w=[R_ACC[ab]])
                    if s == 0:
                        P.op("act", (lambda ab=ab: lambda e: e.activation(out=acc[ab][:], in_=acc[ab][:], func=AF.Silu))(),
                             r=[R_ACC[ab]], w=[R_ACC[ab]])
                P.op("dve", (lambda jj=jj, a0=accs[0], a1=accs[1]: lambda e: e.tensor_tensor(gq[:, jj, :], acc[a0][:], acc[a1][:], ALU.mult))(),
                     r=[R_ACC[accs[0]], R_ACC[accs[1]]], w=[R_GQ[jj]])
            for oc in range(KC):
                wv, wres = ws.get(bi); bi += 1
                u = unit; unit += 1
                pa = 2 * (u % 3)

                def mmd(e, wv=wv, pa=pa):
                    for kc in range(11):
                        e.matmul(ps[:, pa, :], wv[:, kc, :], gq[:, kc, 0:512], start=(kc == 0), stop=(kc == 10))
                        ins = e.matmul(ps[:, pa + 1, :], wv[:, kc, :], gq[:, kc, 512:T], start=(kc == 0), stop=(kc == 10))
                    return ins
                P.op("pe", mmd, r=[wres] + R_GQ, w=[self.R_PS[pa], self.R_PS[pa + 1]])
                P.op("dve", (lambda oc=oc, pa=pa: lambda e: e.tensor_tensor(
                    self.x_res[:, oc, :].rearrange("p (a b) -> p a b", a=2), ps[:, pa:pa + 2, :],
                    self.x_res[:, oc, :].rearrange("p (a b) -> p a b", a=2), ALU.add))(),
                    r=[self.R_PS[pa], self.R_PS[pa + 1], self.R_X[oc]], w=[self.R_X[oc]])

    def phase_final(self):
        self.norm(0, T * 4, final_out=True)

    def phase_conv(self):
        P = self.P
        pv = self.pv_bound()
        ps = self.psum
        dw = self.d_w[self.pi]
        HT = 512
        o = 0
        c = self.sb_at("c", [128, KC, HT], F32, o)
        off_rs = o
        off_sq = o + T * 4
        off_gat = o + T * 4 + 2048
        o += KC * HT * 4
        vst = [self.sb_at(f"vst{i}", [128, HALO + HT], F32, o + i * (HALO + HT) * 4) for i in range(2)]; o += 2 * (HALO + HT) * 4
        sgm = [self.sb_at(f"sgm{i}", [128, HALO + HT], F32, o + i * (HALO + HT) * 4) for i in range(2)]; o += 2 * (HALO + HT) * 4
        m1 = self.sb_at("m1", [128, HT], F32, o); o += HT * 4
        m2 = self.sb_at("m2", [128, HT], F32, o); o += HT * 4
        sqf = [self.sb_at(f"sqf{i}", [128, HT], F32, o + i * HT * 4) for i in range(2)]; o += 2 * HT * 4
        hst = self.sb_at("hst", [128, KC, HALO], BF16, o); o += KC * HALO * 2
        R_C = [Res(f"c{i}") for i in range(KC)]
        R_VST = [Res("vst0"), Res("vst1")]
        R_VSTH = [Res("vsth0"), Res("vsth1")]
        R_SGM = [Res("sgm0"), Res("sgm1")]
        R_SGMH = [Res("sgmh0"), Res("sgmh1")]
        R_M1 = Res("m1"); R_M2 = Res("m2")
        R_SQF = [Res("sqf0"), Res("sqf1")]
        R_HST = Res("hst")
        win = dw["w_in"].rearrange("(k p) n -> p k n", p=128)
        wout = dw["w_out"].rearrange("(k p) n -> p k n", p=128)
        blocks = []
        for hf in range(2):
            for j in range(KC):
                blocks.append(win[:, :, j * 128:(j + 1) * 128])
                blocks.append(win[:, :, D + j * 128:D + (j + 1) * 128])
            for oc in range(KC):
                blocks.append(wout[:, :, oc * 128:(oc + 1) * 128])
        ws = Builder.WStream(self, blocks)
        ws.advance(0)
        self.norm(off_rs, off_sq)
        CUT = int(os.environ.get("CONV_CUT", "99"))
        if CUT == 0:
            return
        self.exchange_halo(CONVK - 1, off_gat)
        NHL = CONVK - 1
        if CUT == 1:
            return
        P.op("act", lambda e: e.activation(out=hst[:], in_=self.hb[:, :, HALO + HT - HALO:HALO + HT], func=AF.Copy), r=self.R_HB, w=[R_HST])
        if CUT == 15:
            return
        bi = 0
        unit = 0
        for hf in range(2):
            base = HALO + hf * HT
            for j in range(KC):
                u = unit; unit += 1
                pa, pg = 2 * (u % 2), 2 * (u % 2) + 1
                ha, hg = 0, 32
                hbk = 6 + (u % 2)
                vb = u % 2
                wva, wra = ws.get(bi); bi += 1
                wvg, wrg = ws.get(bi); bi += 1

                def mm(e, wva=wva, wvg=wvg, pa=pa, pg=pg, ha=ha, hg=hg, hf=hf, base=base, hbk=hbk):
                    for wv, pb, hs in ((wva, pa, ha), (wvg, pg, hg)):
                        for kc in range(KC):
                            hsrc = self.hb[:, kc, HALO - NHL:HALO] if hf == 0 else hst[:, kc, HALO - NHL:HALO]
                            e.matmul(ps[:, pb, :], wv[:, kc, :], self.hb[:, kc, base:base + HT], start=(kc == 0), stop=(kc == KC - 1))
                            ins = e.matmul(ps[:, hbk, hs:hs + NHL], wv[:, kc, :], hsrc, start=(kc == 0), stop=(kc == KC - 1))
                    return ins
                P.op("pe", mm, r=[wra, wrg, self.R_HBH, R_HST] + self.R_HB, w=[self.R_PS[pa], self.R_PS[pg], self.R_PS[hbk]])
                P.op("act", (lambda vb=vb, pg=pg, j=j: lambda e: e.activation(
                    out=sgm[vb][:, HALO:HALO + HT], in_=ps[:, pg, :], func=AF.Sigmoid, bias=pv("b_in", KC + j)))(),
                    r=[self.R_PS[pg], self.cur_pkres], w=[R_SGM[vb]])
                P.op("act", (lambda vb=vb, hg=hg, j=j, hbk=hbk: lambda e: e.activation(
                    out=sgm[vb][:, HALO - NHL:HALO], in_=ps[:, hbk, hg:hg + NHL], func=AF.Sigmoid, bias=pv("b_in", KC + j)))(),
                    r=[self.R_PS[hbk], self.cur_pkres], w=[R_SGMH[vb]])
                P.op("dve", (lambda vb=vb, pa=pa, j=j: lambda e: e.scalar_tensor_tensor(
                    out=vst[vb][:, HALO:HALO + HT], in0=ps[:, pa, :], scalar=pv("b_in", j), in1=sgm[vb][:, HALO:HALO + HT],
                    op0=ALU.add, op1=ALU.mult))(),
                    r=[self.R_PS[pa], R_SGM[vb], self.cur_pkres], w=[R_VST[vb]])
                P.op("dve", (lambda vb=vb, ha=ha, j=j, hbk=hbk: lambda e: e.scalar_tensor_tensor(
                    out=vst[vb][:, HALO - NHL:HALO], in0=ps[:, hbk, ha:ha + NHL], scalar=pv("b_in", j), in1=sgm[vb][:, HALO - NHL:HALO],
                    op0=ALU.add, op1=ALU.mult))(),
                    r=[self.R_PS[hbk], R_SGMH[vb], self.cur_pkres], w=[R_VSTH[vb]])
                if hf == 0:
                    P.op("dve", (lambda vb=vb: lambda e: e.tensor_scalar(
                        vst[vb][:, HALO - NHL:HALO], vst[vb][:, HALO - NHL:HALO], pv("has_prev", 0), None, ALU.mult))(),
                        r=[R_VSTH[vb], self.cur_pkres], w=[R_VSTH[vb]])
                if CUT == 16:
                    return
                for k in range(CONVK - 1, -1, -1):
                    lo = HALO - (CONVK - 1) + k
                    src = vst[vb][:, lo:lo + HT]
                    rr = [R_VST[vb], self.cur_pkres] + ([R_VSTH[vb]] if k < CONVK - 1 else [])
                    if k == CONVK - 1:
                        P.op("dve", (lambda src=src, j=j, k=k: lambda e: e.tensor_scalar(
                            c[:, j, :], src, pv("w_dw", j * CONVK + k), pv("b_dw", j), ALU.mult, ALU.add))(),
                            r=rr, w=[R_C[j]])
                    else:
                        P.op("dve", (lambda src=src, j=j, k=k: lambda e: e.scalar_tensor_tensor(
                            out=c[:, j, :], in0=src, scalar=pv("w_dw", j * CONVK + k), in1=c[:, j, :], op0=ALU.mult, op1=ALU.add))(),
                            r=rr + [R_C[j]], w=[R_C[j]])
            if CUT == 2:
                return
            for j in range(KC):
                P.op("pe", (lambda j=j: lambda e: e.matmul(ps[:, 4, :], self.ones_f[:], c[:, j, :], start=(j == 0), stop=(j == KC - 1)))(),
                     r=[R_C[j], self.R_CONST], w=[self.R_PS[4]])
                b = j % 2
                P.op("act", (lambda j=j, b=b: lambda e: e.activation(out=sqf[b][:], in_=c[:, j, :], func=AF.Square))(),
                     r=[R_C[j]], w=[R_SQF[b]])
                P.op("pe", (lambda j=j, b=b: lambda e: e.matmul(ps[:, 5, :], self.ones_f[:], sqf[b][:], start=(j == 0), stop=(j == KC - 1)))(),
                     r=[R_SQF[b], self.R_CONST], w=[self.R_PS[5]])
            P.op("dve", lambda e: e.tensor_scalar(m1[:], ps[:, 4, :], 1.0 / D, None, ALU.mult), r=[self.R_PS[4]], w=[R_M1])
            P.op("dve", lambda e: e.tensor_tensor(m2[:], m1[:], m1[:], ALU.mult), r=[R_M1], w=[R_M2])
            P.op("dve", lambda e: e.scalar_tensor_tensor(out=m2[:], in0=ps[:, 5, :], scalar=1.0 / D, in1=m2[:], op0=ALU.mult, op1=ALU.subtract),
                 r=[self.R_PS[5], R_M2], w=[R_M2])
            P.op("act", lambda e: e.activation(out=m2[:], in_=m2[:], func=AF.Sqrt, bias=LN_EPS), r=[R_M2], w=[R_M2])
            P.op("dve", lambda e: e.reciprocal(m2[:], m2[:]), r=[R_M2], w=[R_M2])
            for j in range(KC):
                P.op("dve", (lambda j=j: lambda e: e.tensor_tensor(c[:, j, :], c[:, j, :], m1[:], ALU.subtract))(), r=[R_C[j], R_M1], w=[R_C[j]])
                P.op("dve", (lambda j=j: lambda e: e.tensor_tensor(c[:, j, :], c[:, j, :], m2[:], ALU.mult))(), r=[R_C[j], R_M2], w=[R_C[j]])
                P.op("act", (lambda j=j, base=base: lambda e: e.activation(
                    out=self.hb[:, j, base:base + HT], in_=c[:, j, :], func=AF.Silu, scale=pv("ln_g", j), bias=pv("ln_b", j)))(),
                    r=[R_C[j], self.cur_pkres, R_HST], w=[self.R_HB[j]])
            if CUT == 3:
                return
            for oc in range(KC):
                wv, wres = ws.get(bi); bi += 1
                u = unit; unit += 1
                pb = u % 4

                def mmo(e, wv=wv, pb=pb, base=base):
                    for kc in range(KC):
                        ins = e.matmul(ps[:, pb, :], wv[:, kc, :], self.hb[:, kc, base:base + HT], start=(kc == 0), stop=(kc == KC - 1))
                    return ins
                P.op("pe", mmo, r=[wres] + self.R_HB, w=[self.R_PS[pb]])
                xs = self.x_res[:, oc, hf * HT:(hf + 1) * HT]
                P.op("dve", (lambda oc=oc, pb=pb, xs=xs: lambda e: e.scalar_tensor_tensor(
                    out=xs, in0=ps[:, pb, :], scalar=pv("b_out", oc), in1=xs, op0=ALU.add, op1=ALU.add))(),
                    r=[self.R_PS[pb], self.R_X[oc], self.cur_pkres], w=[self.R_X[oc]])

    def phase_ssm(self):
        P = self.P
        pv = self.pv_bound()
        nc = self.nc
        ps = self.psum
        dw = self.d_w[self.pi]
        self.uid += 1
        uid = self.uid
        NT = T // 128
        GW = 512
        d_xsp = nc.dram_tensor(f"xsp{uid}", [128, KC, T], F32).ap()
        d_yloc = nc.dram_tensor(f"yloc{uid}", [NG, NT, 128, GW], F32).ap()
        d_sz = nc.dram_tensor(f"sz{uid}", [NG, 4, 128, T], F32).ap()
        d_c = nc.dram_tensor(f"cfm{uid}", [NG, 128, T], BF16).ap()
        SW = GW + NH
        d_ssnd = nc.dram_tensor(f"ssnd{uid}", [NG, 128, SW], F32).ap()
        d_srcv = nc.dram_tensor(f"srcv{uid}", [NG, 4 * 128, SW], F32).ap()
        R_DX = [Res(f"dx{k}") for k in range(KC)]
        R_DY = Res("dyloc"); R_DSZ = Res("dsz"); R_DC = Res("dc"); R_DSS = [Res(f"dssnd{i}") for i in range(NG)]; R_DSR = [Res(f"dsrcv{i}") for i in range(NG)]
        XB = self.x_off - self.arena_base
        HBB = self.hb_off - self.arena_base
        o = 0

        def A(name, shape, dt):
            nonlocal o
            nbytes = int(np.prod(shape[1:])) * (2 if dt == BF16 else 4)
            t = self.sb_at(name, shape, dt, o)
            o += (nbytes + 31) // 32 * 32
            return t
        IDF, TRIF, UINC, USGT, SELLO, SELHI = self.IDF, self.TRIF, self.UINC, self.USGT, self.SELLO, self.SELHI
        dt_tok = A("dt_tok", [128, NT, NH], F32); a_tok = A("a_tok", [128, NT, NH], F32)
        xb = A("xb", [128, NT, NH], F32); tmp8 = A("tmp8", [128, NT, NH], F32)
        ea = A("ea", [128, NH], F32)
        dec = A("dec", [128, NT, 320], F32)
        dtot = A("dtot", [128, NH], F32)
        onem = A("onem", [128, 4], F32)
        o_keep = o
        B_tok = A("B_tok", [128, NT, 128], BF16); B_bf = A("B_bf", [128, T], BF16); C_bf = [A(f"C_bf{i}", [128, T], BF16) for i in range(2)]
        S = A("S", [128, GW], F32); S_bf = [A(f"S_bf{i}", [128, GW], BF16) for i in range(2)]
        zst = [A(f"zst{i}", [128, T], F32) for i in range(2)]
        off_rs = o; off_sq = o + T * 4; off_gat = o + T * 4 + 2048
        o_after_A = o + T * 4 + 2048 + 1024
        assert self.arena_base + o_after_A <= SBUF_LIMIT
        ox = XB

        def X(name, shape, dt):
            nonlocal ox
            nbytes = int(np.prod(shape[1:])) * (2 if dt == BF16 else 4)
            t = self.sb_at(name, shape, dt, ox)
            ox += (nbytes + 31) // 32 * 32
            assert ox - XB <= KC * T * 4
            return t
        xs_fm = X("xs_fm", [128, 4, T], F32)
        ust = [X(f"ust{i}", [128, HALO + T], F32) for i in range(2)]
        acc = [X(f"acc{i}", [128, T], F32) for i in range(2)]
        xd_tok = [X(f"xd_tok{i}", [128, GW], BF16) for i in range(2)]
        xsD = [X(f"xsD{i}", [128, GW], F32) for i in range(2)]
        xd2_tok = [X(f"xd2_tok{i}", [128, GW], BF16) for i in range(2)]
        Rb = [X(f"R{i}", [128, 8, 128], F32) for i in range(2)]
        Eb = [X(f"E{i}", [128, 8, 128], BF16) for i in range(2)]
        MT = [X(f"MT{i}", [128, 8, 128], BF16) for i in range(2)]
        CBm = [X(f"CBm{i}", [128, 128], F32) for i in range(2)]
        y_tok = [X(f"y_tok{i}", [128, GW], F32) for i in range(2)]
        R = lambda n: Res(n)
        R_CST = self.R_CONST; R_DT = R("dt"); R_AT = R("at"); R_XB = R("xb"); R_TMP8 = R("tmp8"); R_EA = R("ea")
        R_DEC = R("dec"); R_DTOT = R("dtot"); R_ONEM = R("onem")
        R_BTOK = R("btok"); R_BBF = R("bbf"); R_CBF = [R("cbf0"), R("cbf1")]
        R_S = R("S"); R_SBF = [R("sbf0"), R("sbf1")]
        R_ZST = [R("zst0"), R("zst1")]
        R_XSFM = [R(f"xsfm{i}") for i in range(4)]
        R_UST = [R("ust0"), R("ust1")]; R_USTH = [R("usth0"), R("usth1")]
        R_ACC = [R("acc0"), R("acc1")]
        R_XD = [R("xd0"), R("xd1")]; R_XSD = [R("xsD0"), R("xsD1")]; R_XD2 = [R("xd20"), R("xd21")]
        R_R = [R("R0"), R("R1")]; R_E = [R("E0"), R("E1")]; R_MT = [R("MT0"), R("MT1")]; R_CBM = [R("CBm0"), R("CBm1")]
        R_YT = [R("yt0"), R("yt1")]
        win = dw["w_in"].rearrange("(k p) n -> p k n", p=128)
        wout = dw["w_out"].rearrange("(k p) n -> p k n", p=128)
        blocks = [win[:, :, DI + DI + 2 * NG * DSTATE:SSM_IN]]
        for g in range(NG):
            for chn in range(4):
                blocks.append(win[:, :, g * GW + chn * 128:g * GW + (chn + 1) * 128])
            for chn in range(4):
                blocks.append(win[:, :, DI + g * GW + chn * 128:DI + g * GW + (chn + 1) * 128])
            blocks.append(win[:, :, 2 * DI + g * 128:2 * DI + (g + 1) * 128])
            blocks.append(win[:, :, 2 * DI + NG * DSTATE + g * 128:2 * DI + NG * DSTATE + (g + 1) * 128])
        for oc in range(KC):
            for hk in range(2):
                blocks.append(wout[:, hk * 16:(hk + 1) * 16, oc * 128:(oc + 1) * 128])
        ws = Builder.WStream(self, blocks)
        ws.advance(0)
        self.norm(off_rs, off_sq)
        self.exchange_halo(3, off_gat)
        P.op("sp", lambda e: e.dma_start(out=d_xsp, in_=self.x_res[:]), r=list(self.R_X), w=list(R_DX), chan=self.nch())
        P.op("dve", lambda e: e.tensor_scalar(onem[:], pv("prev_mask", 0, 4), -1.0, 1.0, ALU.mult, ALU.add), r=[self.cur_pkres], w=[R_ONEM])
        wv, wres = ws.get(0)

        def mm_dt(e, wv=wv):
            for tt in range(NT):
                for kc in range(KC):
                    ins = e.matmul(ps[:, 4, tt * NH:(tt + 1) * NH], self.hb[:, kc, HALO + tt * 128:HALO + (tt + 1) * 128], wv[:, kc, :],
                                   start=(kc == 0), stop=(kc == KC - 1))
            return ins
        P.op("pe", mm_dt, r=[wres] + self.R_HB, w=[self.R_PS[4]])
        ps4v = ps[:, 4, :].rearrange("p (t h) -> p t h", t=NT)
        P.op("dve", lambda e: e.tensor_tensor(xb[:], ps4v, pv("dt_bias", 0, NH).unsqueeze(1).to_broadcast([128, NT, NH]), ALU.add),
             r=[self.R_PS[4], self.cur_pkres], w=[R_XB])
        P.op("act", lambda e: e.activation(out=tmp8[:], in_=xb[:], func=AF.Abs), r=[R_XB], w=[R_TMP8])
        P.op("act", lambda e: e.activation(out=tmp8[:], in_=tmp8[:], func=AF.Exp, scale=-1.0), r=[R_TMP8], w=[R_TMP8])
        P.op("act", lambda e: e.activation(out=tmp8[:], in_=tmp8[:], func=AF.Ln, bias=1.0), r=[R_TMP8], w=[R_TMP8])
        P.op("dve", lambda e: e.scalar_tensor_tensor(out=dt_tok[:], in0=xb[:], scalar=0.0, in1=tmp8[:], op0=ALU.max, op1=ALU.add),
             r=[R_XB, R_TMP8], w=[R_DT])
        P.op("act", lambda e: e.activation(out=ea[:], in_=pv("a_log", 0, NH), func=AF.Exp), r=[self.cur_pkres], w=[R_EA])
        P.op("dve", lambda e: e.scalar_tensor_tensor(out=a_tok[:], in0=dt_tok[:], scalar=-1.0, in1=ea[:].unsqueeze(1).to_broadcast([128, NT, NH]),
                                                      op0=ALU.mult, op1=ALU.mult), r=[R_DT, R_EA], w=[R_AT])
        for tt in range(NT):
            def mm_dec(e, tt=tt):
                e.matmul(ps[:, 5, 0:64], UINC[:], a_tok[:, tt, :], start=True, stop=True)
                e.matmul(ps[:, 5, 64:128], USGT[:], a_tok[:, tt, :], start=True, stop=True)
                e.matmul(ps[:, 5, 128:192], SELLO[:], a_tok[:, tt, :], start=True, stop=True)
                ins = e.matmul(ps[:, 5, 192:256], SELHI[:], a_tok[:, tt, :], start=True, stop=True)
                for t2 in range(tt):
                    e.matmul(ps[:, 5, 256:320], self.ones_f[:], a_tok[:, t2, :], start=(t2 == 0), stop=False)
                ins = e.matmul(ps[:, 5, 256:320], TRIF[:], a_tok[:, tt, :], start=(tt == 0), stop=True)
                return ins
            P.op("pe", mm_dec, r=[R_AT, R_CST, self.R_CONST], w=[self.R_PS[5]])
            P.op("act", (lambda tt=tt: lambda e: e.activation(out=dec[:, tt, :], in_=ps[:, 5, 0:320], func=AF.Exp))(),
                 r=[self.R_PS[5]], w=[R_DEC])

        def mm_tot(e):
            for t2 in range(NT):
                ins = e.matmul(ps[:, 5, 320:384], self.ones_f[:], a_tok[:, t2, :], start=(t2 == 0), stop=(t2 == NT - 1))
            return ins
        P.op("pe", mm_tot, r=[R_AT, self.R_CONST], w=[self.R_PS[5]])
        P.op("act", lambda e: e.activation(out=dtot[:], in_=ps[:, 5, 320:384], func=AF.Exp), r=[self.R_PS[5]], w=[R_DTOT])
        SCUT = int(os.environ.get("SSM_CUT", "99"))

        def bail():
            P.barrier()
            P.op("sp", lambda e: e.dma_start(out=self.x_res[:], in_=d_xsp), r=list(R_DX), w=list(self.R_X), chan=self.nch())
        P.barrier()
        if SCUT == 1:
            return bail()
        st = {'bi': 1, 'unit': 0}

        def do_group(g):
            hsl = slice(g * 8, (g + 1) * 8)
            cb = g % 2
            bc = lambda ap: ap.unsqueeze(2).to_broadcast([128, 8, 64])
            ps4g = ps[:, 4, :].rearrange("p (r q) -> p r q", r=8)
            for kind in ("z", "z", "z", "z", "x", "x", "x", "x", "B", "C"):
                wv, wres = ws.get(st['bi'])
                chn = (st['bi'] - 1) % 10
                st['bi'] += 1
                u = st['unit']; st['unit'] += 1
                pa = 2 * (u % 2)
                hbk = 6 + (u % 2)
                ub = u % 2
                has_halo = kind != "z"

                def mm(e, wv=wv, pa=pa, hbk=hbk, has_halo=has_halo):
                    for kc in range(KC):
                        e.matmul(ps[:, pa, :], wv[:, kc, :], self.hb[:, kc, HALO:HALO + 512], start=(kc == 0), stop=(kc == KC - 1))
                        ins = e.matmul(ps[:, pa + 1, :], wv[:, kc, :], self.hb[:, kc, HALO + 512:HALO + T], start=(kc == 0), stop=(kc == KC - 1))
                        if has_halo:
                            ins = e.matmul(ps[:, hbk, 0:3], wv[:, kc, :], self.hb[:, kc, HALO - 3:HALO], start=(kc == 0), stop=(kc == KC - 1))
                    return ins
                P.op("pe", mm, r=[wres, self.R_HBH] + self.R_HB, w=[self.R_PS[pa], self.R_PS[pa + 1]] + ([self.R_PS[hbk]] if has_halo else []))
                pspair = ps[:, pa:pa + 2, :]
                if kind == "z":
                    zb = (g * 4 + chn) % 2
                    P.op("act", (lambda zb=zb, pspair=pspair: lambda e: e.activation(
                        out=zst[zb][:].rearrange("p (a b) -> p a b", a=2), in_=pspair, func=AF.Silu))(),
                        r=[self.R_PS[pa], self.R_PS[pa + 1]], w=[R_ZST[zb]])
                    P.op("sp", (lambda zb=zb, g=g, chn=chn: lambda e: e.dma_start(out=d_sz[g, chn], in_=zst[zb][:]))(),
                         r=[R_ZST[zb]], w=[R_DSZ], chan=self.nch())
                    continue
                P.op("act", (lambda ub=ub, pspair=pspair: lambda e: e.activation(
                    out=ust[ub][:, HALO:HALO + T].rearrange("p (a b) -> p a b", a=2), in_=pspair, func=AF.Copy))(),
                    r=[self.R_PS[pa], self.R_PS[pa + 1]], w=[R_UST[ub]])
                P.op("act", (lambda ub=ub, hbk=hbk: lambda e: e.activation(out=ust[ub][:, HALO - 3:HALO], in_=ps[:, hbk, 0:3], func=AF.Copy))(),
                     r=[self.R_PS[hbk]], w=[R_USTH[ub]])
                if kind == "x":
                    cidx = g * 4 + (chn - 4)
                    dst = xs_fm[:, chn - 4, :]
                    rdst = R_XSFM[chn - 4]
                elif kind == "B":
                    cidx = 32 + g
                    dst = acc[0][:]
                    rdst = R_ACC[0]
                else:
                    cidx = 40 + g
                    dst = acc[1][:]
                    rdst = R_ACC[1]
                P.op("dve", (lambda ub=ub, dst=dst, cidx=cidx: lambda e: e.tensor_scalar(
                    dst, ust[ub][:, HALO:HALO + T], pv("w_conv", cidx * 4 + 3), pv("b_conv", cidx), ALU.mult, ALU.add))(),
                    r=[R_UST[ub], self.cur_pkres], w=[rdst])
                for k in (2, 1, 0):
                    sh = 3 - k
                    P.op("dve", (lambda ub=ub, dst=dst, cidx=cidx, k=k, sh=sh: lambda e: e.scalar_tensor_tensor(
                        out=dst, in0=ust[ub][:, HALO - sh:HALO - sh + T], scalar=pv("w_conv", cidx * 4 + k), in1=dst,
                        op0=ALU.mult, op1=ALU.add))(), r=[R_UST[ub], R_USTH[ub], self.cur_pkres, rdst], w=[rdst])
                if kind == "x":
                    P.op("act", (lambda dst=dst: lambda e: e.activation(out=dst, in_=dst, func=AF.Silu))(), r=[rdst], w=[rdst])
                elif kind == "B":
                    P.op("act", lambda e: e.activation(out=acc[0][:], in_=acc[0][:], func=AF.Silu), r=[R_ACC[0]], w=[R_ACC[0]])
                    P.op("act", lambda e: e.activation(out=B_bf[:], in_=acc[0][:], func=AF.Copy), r=[R_ACC[0]], w=[R_BBF])
                    for tt in range(NT):
                        P.op("pe", (lambda tt=tt: lambda e: e.transpose(ps[:, 5, 0:128], acc[0][:, tt * 128:(tt + 1) * 128], IDF[:]))(),
                             r=[R_ACC[0], R_CST], w=[self.R_PS[5]])
                        P.op("act", (lambda tt=tt: lambda e: e.activation(out=B_tok[:, tt, :], in_=ps[:, 5, 0:128], func=AF.Copy))(),
                             r=[self.R_PS[5]], w=[R_BTOK])
                else:
                    cb = g % 2
                    P.op("act", (lambda cb=cb: lambda e: e.activation(out=C_bf[cb][:], in_=acc[1][:], func=AF.Silu))(), r=[R_ACC[1]], w=[R_CBF[cb]])
                    P.op("sp", (lambda cb=cb, g=g: lambda e: e.dma_start(out=d_c[g], in_=C_bf[cb][:]))(), r=[R_CBF[cb]], w=[R_DC], chan=self.nch())
            P.op("dve", lambda e: e.memset(S[:], 0.0), w=[R_S])
            P.op("dve", lambda e: e.memset(S_bf[0][:], 0.0), w=[R_SBF[0]])
            cst = {'cur': 0}

            def do_tile(tt):
                b = tt % 2
                tsl = slice(tt * 128, (tt + 1) * 128)

                def mm_tr(e, tsl=tsl):
                    for chn in range(4):
                        ins = e.transpose(ps[:, 4, chn * 128:(chn + 1) * 128], xs_fm[:, chn, tsl], IDF[:])
                    return ins
                P.op("pe", mm_tr, r=R_XSFM + [R_CST], w=[self.R_PS[4]])
                P.op("dve", (lambda b=b, tt=tt: lambda e: e.tensor_tensor(
                    xd_tok[b][:].rearrange("p (r q) -> p r q", r=8), ps4g, bc(dt_tok[:, tt, hsl]), ALU.mult))(),
                    r=[self.R_PS[4], R_DT], w=[R_XD[b]])
                P.op("dve", (lambda b=b: lambda e: e.tensor_tensor(
                    xsD[b][:].rearrange("p (r q) -> p r q", r=8), ps4g, bc(pv("d_skip", g * 8, 8)), ALU.mult))(),
                    r=[self.R_PS[4], self.cur_pkres], w=[R_XSD[b]])
                P.op("dve", (lambda b=b, tt=tt: lambda e: e.tensor_tensor(
                    xd2_tok[b][:].rearrange("p (r q) -> p r q", r=8), xd_tok[b][:].rearrange("p (r q) -> p r q", r=8),
                    bc(dec[:, tt, 64 + g * 8:64 + g * 8 + 8]), ALU.mult))(), r=[R_XD[b], R_DEC], w=[R_XD2[b]])
                P.op("pe", (lambda tsl=tsl: lambda e: e.matmul(ps[:, 5, 0:128], B_bf[:, tsl], C_bf[cb][:, tsl], start=True, stop=True))(),
                     r=[R_BBF, R_CBF[cb]], w=[self.R_PS[5]])
                P.op("dve", (lambda b=b: lambda e: e.tensor_tensor(CBm[b][:], ps[:, 5, 0:128], UINC[:], ALU.mult))(),
                     r=[self.R_PS[5], R_CST], w=[R_CBM[b]])
                P.op("dve", (lambda b=b, tt=tt: lambda e: e.tensor_tensor(
                    Rb[b][:], a_tok[:, tt, hsl].unsqueeze(2).to_broadcast([128, 8, 128]),
                    UINC[:].unsqueeze(1).to_broadcast([128, 8, 128]), ALU.mult))(), r=[R_AT, R_CST], w=[R_R[b]])

                def mm_seg(e, b=b):
                    e.matmul(ps[:, 0, :], USGT[:], Rb[b][:, 0:4, :], start=True, stop=True)
                    return e.matmul(ps[:, 1, :], USGT[:], Rb[b][:, 4:8, :], start=True, stop=True)
                P.op("pe", mm_seg, r=[R_R[b], R_CST], w=[self.R_PS[0], self.R_PS[1]])
                P.op("act", (lambda b=b: lambda e: e.activation(out=Eb[b][:].rearrange("p (a r) l -> p a (r l)", a=2),
                                                                  in_=ps[:, 0:2, :], func=AF.Exp))(),
                     r=[self.R_PS[0], self.R_PS[1]], w=[R_E[b]])
                P.op("dve", (lambda b=b: lambda e: e.tensor_tensor(
                    MT[b][:], Eb[b][:], CBm[b][:].unsqueeze(1).to_broadcast([128, 8, 128]), ALU.mult))(),
                    r=[R_E[b], R_CBM[b]], w=[R_MT[b]])

                def mm_yd(e, b=b):
                    for r_ in range(8):
                        ins = e.matmul(ps[:, 2, r_ * 64:(r_ + 1) * 64], MT[b][:, r_, :], xd_tok[b][:, r_ * 64:(r_ + 1) * 64], start=True, stop=True)
                    return ins
                P.op("pe", mm_yd, r=[R_MT[b], R_XD[b]], w=[self.R_PS[2]])
                def do_chunk(hh):
                    psl = slice(hh * 64, (hh + 1) * 64)
                    csl = slice(tt * 128 + hh * 64, tt * 128 + (hh + 1) * 64)
                    sbank = 6 + hh
                    cdo = 128 + hh * 64 + g * 8
                    cur = cst['cur']

                    def mm_off(e, psl=psl, csl=csl, cur=cur, b=b, sbank=sbank, tt=tt):
                        e.matmul(ps[psl, 3, :], C_bf[cb][:, csl], S_bf[cur][:], start=True, stop=True)
                        return e.matmul(ps[:, sbank, :], B_tok[psl, tt, :], xd2_tok[b][psl, :], start=True, stop=True)
                    P.op("pe", mm_off, r=[R_CBF[cb], R_SBF[cur], R_BTOK, R_XD2[b]], w=[self.R_PS[3], self.R_PS[sbank]])
                    P.op("dve", (lambda cdo=cdo, tt=tt: lambda e: e.tensor_tensor(
                        S[:].rearrange("p (r q) -> p r q", r=8), S[:].rearrange("p (r q) -> p r q", r=8),
                        bc(dec[:, tt, cdo:cdo + 8]), ALU.mult))(), r=[R_S, R_DEC], w=[R_S])
                    P.op("dve", (lambda sbank=sbank: lambda e: e.tensor_tensor(S[:], S[:], ps[:, sbank, :], ALU.add))(),
                         r=[R_S, self.R_PS[sbank]], w=[R_S])
                    cur = 1 - cur
                    cst['cur'] = cur
                    P.op("act", (lambda cur=cur: lambda e: e.activation(out=S_bf[cur][:], in_=S[:], func=AF.Copy))(), r=[R_S], w=[R_SBF[cur]])
                do_chunk(0)
                do_chunk(1)
                P.op("dve", (lambda b=b, tt=tt: lambda e: e.tensor_tensor(
                    y_tok[b][:].rearrange("p (r q) -> p r q", r=8), ps[:, 3, :].rearrange("p (r q) -> p r q", r=8),
                    bc(dec[:, tt, g * 8:g * 8 + 8]), ALU.mult))(), r=[self.R_PS[3], R_DEC], w=[R_YT[b]])
                P.op("dve", (lambda b=b: lambda e: e.tensor_tensor(y_tok[b][:], y_tok[b][:], ps[:, 2, :], ALU.add))(),
                     r=[R_YT[b], self.R_PS[2]], w=[R_YT[b]])
                P.op("dve", (lambda b=b: lambda e: e.tensor_tensor(y_tok[b][:], y_tok[b][:], xsD[b][:], ALU.add))(),
                     r=[R_YT[b], R_XSD[b]], w=[R_YT[b]])
                P.op("sp", (lambda b=b, g=g, tt=tt: lambda e: e.dma_start(out=d_yloc[g, tt], in_=y_tok[b][:]))(),
                     r=[R_YT[b]], w=[R_DY], chan=self.nch())
            for tt in range(NT):
                do_tile(tt)
            P.op("sp", (lambda g=g: lambda e: e.dma_start(out=d_ssnd[g][:, 0:GW], in_=S[:]))(), r=[R_S], w=[R_DSS[g]], chan=self.nch())
            P.op("sp", (lambda g=g: lambda e: e.dma_start(out=d_ssnd[g][:, GW:SW], in_=dtot[:]))(), r=[R_DTOT, R_DSS[g]], w=[R_DSS[g]], chan=self.nch())
            P.op("pool", (lambda g=g: lambda e: e.collective_compute("AllGather", ALU.bypass, replica_groups=[[0, 1, 2, 3], [4, 5, 6, 7]],
                                                                      ins=[d_ssnd[g]], outs=[d_srcv[g]]))(),
                 r=[R_DSS[g]], w=[R_DSR[g]], chan=self.ch_cc)
        for g in range(NG):
            do_group(g)
            if SCUT == 3 and g == 0:
                return bail()
        bi = st['bi']
        if SCUT == 4:
            return bail()
        P.barrier()
        if SCUT == 5:
            return bail()
        o = o_keep
        Sj = A("Sj", [128, 3, GW], F32); Dj = A("Dj", [128, 3, NH], F32); S0 = A("S0", [128, GW], F32); dpr = A("dpr", [128, 8], F32)
        S0_bf = [A(f"S0_bf{i}", [128, GW], BF16) for i in range(2)]
        Cb = [A(f"Cb{i}", [128, T], BF16) for i in range(2)]
        o_io = o
        yl = [A(f"yl{i}", [128, GW], F32) for i in range(2)]
        szb = [A(f"szb{i}", [128, T], F32) for i in range(2)]
        rs = A("rsg", [128, T], F32)
        sq = [A(f"sqg{i}", [128, 512], BF16) for i in range(2)]
        assert self.arena_base + o <= SBUF_LIMIT, o
        yf = self.sb_at("yf", [128, NT, GW], F32, HBB)
        yg = self.sb_at("yg", [128, 4, T], F32, HBB + NT * GW * 4)
        assert NT * GW * 4 + 4 * T * 4 <= KC * (HALO + T) * 2
        yn = self.sb_at("yn", [128, 32, T], BF16, XB)
        R_SJ = R("Sj"); R_DJ = R("Dj"); R_S0 = R("S0"); R_DPR = R("dpr"); R_S0BF = [R("s0bf0"), R("s0bf1")]
        R_CB = [R("cb0"), R("cb1")]; R_YL = [R("yl0"), R("yl1")]; R_SZB = [R("szb0"), R("szb1")]
        R_RSG = R("rsg"); R_SQG = [R("sqg0"), R("sqg1")]
        R_YF = [R(f"yf{i}") for i in range(NT)]; R_YG = [R(f"yg{i}") for i in range(4)]; R_YN = [R(f"yn{i}") for i in range(32)]
        srv = lambda g: d_srcv[g].rearrange("(r p) n -> p r n", p=128)
        P.op("sp", lambda e: e.dma_start(out=Dj[:], in_=srv(0)[:, 0:3, GW:SW]), r=[R_DSR[0]], w=[R_DJ], chan=self.nch())
        pm = lambda j: pv("prev_mask", j)
        bc = lambda ap: ap.unsqueeze(2).to_broadcast([128, 8, 64])
        g8 = lambda t: t.rearrange("p (r q) -> p r q", r=8)
        for g in range(NG):
            gb = g % 2
            P.op("sp", (lambda g=g: lambda e: e.dma_start(out=Sj[:], in_=srv(g)[:, 0:3, 0:GW]))(), r=[R_DSR[g]], w=[R_SJ], chan=self.nch())
            P.op("dve", lambda e: e.tensor_scalar(S0[:], Sj[:, 0, :], pm(0), None, ALU.mult), r=[R_SJ, self.cur_pkres], w=[R_S0])
            for j in (1, 2):
                P.op("dve", (lambda j=j, g=g: lambda e: e.tensor_scalar(dpr[:], Dj[:, j, g * 8:(g + 1) * 8], pm(j), onem[:, j:j + 1], ALU.mult, ALU.add))(),
                     r=[R_DJ, R_ONEM, self.cur_pkres], w=[R_DPR])
                P.op("dve", lambda e: e.tensor_tensor(g8(S0[:]), g8(S0[:]), bc(dpr[:]), ALU.mult), r=[R_S0, R_DPR], w=[R_S0])
                P.op("dve", (lambda j=j: lambda e: e.scalar_tensor_tensor(out=S0[:], in0=Sj[:, j, :], scalar=pm(j), in1=S0[:], op0=ALU.mult, op1=ALU.add))(),
                     r=[R_SJ, R_S0, self.cur_pkres], w=[R_S0])
            P.op("act", (lambda gb=gb: lambda e: e.activation(out=S0_bf[gb][:], in_=S0[:], func=AF.Copy))(), r=[R_S0], w=[R_S0BF[gb]])
            P.op("sp", (lambda g=g, gb=gb: lambda e: e.dma_start(out=Cb[gb][:], in_=d_c[g]))(), r=[R_DC], w=[R_CB[gb]], chan=self.nch())
            for tt in range(NT):
                b = tt % 2
                tsl = slice(tt * 128, (tt + 1) * 128)
                P.op("sp", (lambda g=g, tt=tt, b=b: lambda e: e.dma_start(out=yl[b][:], in_=d_yloc[g, tt]))(), r=[R_DY], w=[R_YL[b]], chan=self.nch())
                P.op("pe", (lambda gb=gb, tsl=tsl, b=b: lambda e: e.matmul(ps[:, 2 + b, :], Cb[gb][:, tsl], S0_bf[gb][:], start=True, stop=True))(),
                     r=[R_CB[gb], R_S0BF[gb]], w=[self.R_PS[2 + b]])
                P.op("dve", (lambda tt=tt, b=b, g=g: lambda e: e.tensor_tensor(
                    g8(yf[:, tt, :]), g8(ps[:, 2 + b, :]), bc(dec[:, tt, 256 + g * 8:256 + g * 8 + 8]), ALU.mult))(),
                    r=[self.R_PS[2 + b], R_DEC], w=[R_YF[tt]])
                P.op("dve", (lambda tt=tt, b=b: lambda e: e.tensor_tensor(yf[:, tt, :], yf[:, tt, :], yl[b][:], ALU.add))(),
                     r=[R_YF[tt], R_YL[b]], w=[R_YF[tt]])
            for chn in range(4):
                zb = chn % 2
                pa = 0 if chn % 2 == 0 else 4
                P.op("sp", (lambda g=g, chn=chn, zb=zb: lambda e: e.dma_start(out=szb[zb][:], in_=d_sz[g, chn]))(), r=[R_DSZ], w=[R_SZB[zb]], chan=self.nch())

                def mm_tb(e, chn=chn, pa=pa):
                    for tt in range(NT):
                        ins = e.transpose(ps[:, pa + tt // 4, (tt % 4) * 128:(tt % 4 + 1) * 128], yf[:, tt, chn * 128:(chn + 1) * 128], IDF[:])
                    return ins
                P.op("pe", mm_tb, r=R_YF + [R_CST], w=[self.R_PS[pa], self.R_PS[pa + 1]])
                P.op("dve", (lambda chn=chn, pa=pa, zb=zb: lambda e: e.tensor_tensor(
                    yg[:, chn, :].rearrange("p (a b) -> p a b", a=2), ps[:, pa:pa + 2, :], szb[zb][:].rearrange("p (a b) -> p a b", a=2), ALU.mult))(),
                    r=[self.R_PS[pa], self.R_PS[pa + 1], R_SZB[zb]], w=[R_YG[chn]])
            for hfc in range(2):
                bank = 6 + hfc
                cs = slice(hfc * 512, (hfc + 1) * 512)
                for chn in range(4):
                    b = chn % 2
                    P.op("act", (lambda chn=chn, b=b, cs=cs: lambda e: e.activation(out=sq[b][:], in_=yg[:, chn, cs], func=AF.Square))(),
                         r=[R_YG[chn]], w=[R_SQG[b]])
                    P.op("pe", (lambda chn=chn, b=b, bank=bank: lambda e: e.matmul(ps[:, bank, :], self.ones_bf[:], sq[b][:], start=(chn == 0), stop=(chn == 3)))(),
                         r=[R_SQG[b], self.R_CONST], w=[self.R_PS[bank]])
                P.op("act", (lambda cs=cs, bank=bank: lambda e: e.activation(out=rs[:, cs], in_=ps[:, bank, :], func=AF.Sqrt, scale=1.0 / GW, bias=RMS_EPS))(),
                     r=[self.R_PS[bank]], w=[R_RSG])
            P.op("dve", lambda e: e.reciprocal(rs[:], rs[:]), r=[R_RSG], w=[R_RSG])
            for chn in range(4):
                P.op("dve", (lambda chn=chn, g=g: lambda e: e.scalar_tensor_tensor(
                    out=yn[:, g * 4 + chn, :], in0=yg[:, chn, :], scalar=pv("gn_g", g * 4 + chn), in1=rs[:], op0=ALU.mult, op1=ALU.mult))(),
                    r=[R_YG[chn], R_RSG, self.cur_pkres], w=[R_YN[g * 4 + chn]])
        if SCUT == 6:
            return bail()
        P.barrier()
        o = o_io
        xo = [A(f"xo{i}", [128, T], F32) for i in range(2)]
        xn = [A(f"xn{i}", [128, T], F32) for i in range(2)]
        R_XO = [R("xo0"), R("xo1")]; R_XN = [R("xn0"), R("xn1")]
        for oc in range(KC):
            b = oc % 2
            pa = 2 * (oc % 2)
            wv0, wr0 = ws.get(bi); bi += 1
            wv1, wr1 = ws.get(bi); bi += 1
            P.op("sp", (lambda oc=oc, b=b: lambda e: e.dma_start(out=xo[b][:], in_=d_xsp[:, oc, :]))(), r=[R_DX[oc]], w=[R_XO[b]], chan=self.nch())

            def mmo(e, wv0=wv0, wv1=wv1, pa=pa):
                for hk, wv in ((0, wv0), (1, wv1)):
                    for kc in range(16):
                        st = (hk == 0 and kc == 0)
                        sp_ = (hk == 1 and kc == 15)
                        e.matmul(ps[:, pa, :], wv[:, kc, :], yn[:, hk * 16 + kc, 0:512], start=st, stop=sp_)
                        ins = e.matmul(ps[:, pa + 1, :], wv[:, kc, :], yn[:, hk * 16 + kc, 512:T], start=st, stop=sp_)
                return ins
            P.op("pe", mmo, r=[wr0, wr1] + R_YN, w=[self.R_PS[pa], self.R_PS[pa + 1]])
            P.op("dve", (lambda b=b, pa=pa: lambda e: e.tensor_tensor(
                xn[b][:].rearrange("p (a b) -> p a b", a=2), ps[:, pa:pa + 2, :], xo[b][:].rearrange("p (a b) -> p a b", a=2), ALU.add))(),
                r=[self.R_PS[pa], self.R_PS[pa + 1], R_XO[b]], w=[R_XN[b]])
            P.op("sp", (lambda oc=oc, b=b: lambda e: e.dma_start(out=d_xsp[:, oc, :], in_=xn[b][:]))(), r=[R_XN[b]], w=[R_DX[oc]], chan=self.nch())
            if SCUT == 7 and oc == 1:
                break
            if SCUT == 8 and oc == 7:
                break
        P.barrier()
        P.op("sp", lambda e: e.dma_start(out=self.x_res[:], in_=d_xsp), r=list(R_DX), w=list(self.R_X), chan=self.nch())


class _Dummy(dict):
    SH = {
        "norm_mix_g": (4, D), "norm_ffn_g": (4, D), "norm_final_g": (D,),
        "cv_b_in": (2, 2 * D), "cv_w_dw": (2, CONVK, D), "cv_b_dw": (2, D), "cv_ln_g": (2, D), "cv_ln_b": (2, D),
        "cv_b_out": (2, D), "ssm_w_conv": (2, 4, DI + 2 * NG * DSTATE), "ssm_b_conv": (2, DI + 2 * NG * DSTATE),
        "ssm_dt_bias": (2, NH), "ssm_a_log": (2, NH), "ssm_d": (2, NH), "ssm_norm_g": (2, DI),
        "ffn_w_dw": (4, 3, 2 * FF), "ffn_b_dw": (4, 2 * FF),
    }

    def __getitem__(self, k):
        return np.zeros(self.SH[k], np.float32)


DUMMY_INP = _Dummy()


def to_fm(x_core):
    return np.ascontiguousarray(x_core.T.reshape(KC, 128, T).transpose(1, 0, 2))


def from_fm(y):
    return np.ascontiguousarray(y.transpose(1, 0, 2).reshape(D, T).T)


_NC_CACHE = {}


def run_phases(phases, xs, inp):
    key = tuple(p[0] for p in phases)
    if key not in _NC_CACHE:
        b = Builder(phases)
        _NC_CACHE[key] = b.build()
    nc = _NC_CACHE[key]
    in_maps = []
    for c in range(NCORES):
        m = {"xin": xs[c]}
        pk = np.zeros((128, len(phases), PK_MAX), np.float32)
        for i, ph in enumerate(phases):
            a = pack_phase(inp, ph, c).build()
            pk[:, i, :a.shape[1]] = a
            for wn, (src, idx) in phase_weight_names(ph).items():
                m[f"{wn}_{i}"] = inp[src][idx]
        m["params"] = pk
        in_maps.append(m)
    res = run_bass_kernel_spmd(nc, in_maps, core_ids=list(range(NCORES)))
    return [res.results[c]["xout"] for c in range(NCORES)]


LAUNCHES = [
    [("conv", 0), ("ffn", 0), ("ssm", 1), ("ffn", 1), ("conv", 2), ("ffn", 2), ("ssm", 3), ("ffn", 3), ("final", 0)],
]


def kernel(**inp):
    inp = {k: np.asarray(v) for k, v in inp.items()}
    x = inp["x"]
    xs = [to_fm(x[c // 4, (c % 4) * T:(c % 4 + 1) * T, :]) for c in range(NCORES)]
    for phases in LAUNCHES:
        xs = run_phases(phases, xs, inp)
    out = np.empty_like(x)
    for c in range(NCORES):
        out[c // 4, (c % 4) * T:(c % 4 + 1) * T, :] = from_fm(xs[c])
    return out
```
